# Optimizing a Trainium2 kernel written in Bass

```python
import math
import jax, jax.numpy as jnp
from jax import lax
import numpy as np

D_MODEL = 4096
BATCH = 2
SEQ = 8192
DEPTH = 2

GRID_W = 64
CTX_LEN = 256
D_MIX = D_MODEL
W_CONV = D_MIX // 4
W_ATT = D_MIX // 2
W_SSM = D_MIX - W_CONV - W_ATT
ATT_HEAD_DIM = 64
ATT_HEADS = W_ATT // (2 * ATT_HEAD_DIM)
SSM_GROUP = 16
SSM_GROUPS = W_SSM // SSM_GROUP
SSM_STATE = 64
D_FF = 2 * D_MODEL
Q_BLOCK = 128
ROPE_BASE = 10000.0
NORM_EPS = 1e-6
N_MOD = 9

OFF_CONV_H = 0
OFF_CONV_B = W_CONV
OFF_CONV_C = 2 * W_CONV
OFF_Q = 3 * W_CONV
OFF_K = OFF_Q + W_ATT
OFF_V = OFF_K + W_ATT
OFF_U = OFF_V + W_ATT
D_IN_PROJ = OFF_U + W_SSM

kernel_name = "hybrid_conv_diffattn_s5_macaron_dit"


def rmsnorm(x, gain=None):
    x32 = x.astype(jnp.float32)
    y = x32 * lax.rsqrt(jnp.mean(x32 * x32, axis=-1, keepdims=True) + NORM_EPS)
    if gain is not None:
        y = y * gain.astype(jnp.float32)
    return y.astype(x.dtype)


def modulate(x, shift, scale):
    return rmsnorm(x) * (1.0 + scale) + shift


def swiglu(x, w1, w3, w2):
    return (jax.nn.silu(x @ w1) * (x @ w3)) @ w2


def short_conv(h, b_gate, c_gate, conv_w):
    z = c_gate * h
    zp = jnp.pad(z, ((0, 0), (1, 1), (0, 0)))
    zc = zp[:, :-2] * conv_w[0] + zp[:, 1:-1] * conv_w[1] + zp[:, 2:] * conv_w[2]
    return b_gate * zc


def axial_rope_tables(rows):
    row = jnp.repeat(jnp.arange(rows, dtype=jnp.int32), GRID_W).astype(jnp.float32)
    col = jnp.tile(jnp.arange(GRID_W, dtype=jnp.int32), rows).astype(jnp.float32)
    half = ATT_HEAD_DIM // 2
    inv = ROPE_BASE ** (-jnp.arange(0, half, 2, dtype=jnp.float32) / half)
    ang_r = row[:, None] * inv[None]
    ang_c = col[:, None] * inv[None]
    ang = jnp.concatenate([ang_r, ang_r, ang_c, ang_c], axis=-1)
    return jnp.cos(ang), jnp.sin(ang)


def rotate_axial_half(t):
    a1, a2, b1, b2 = jnp.split(t, 4, axis=-1)
    return jnp.concatenate([-a2, a1, -b2, b1], axis=-1)


def apply_axial_rope(t, cos, sin):
    cos = cos[None, :, None, None, :].astype(t.dtype)
    sin = sin[None, :, None, None, :].astype(t.dtype)
    return t * cos + rotate_axial_half(t) * sin


def diff_attend(q, k, v, lam):
    s = jnp.einsum('bqhjd,bkhjd->jbhqk', q, k).astype(jnp.float32) * (ATT_HEAD_DIM ** -0.5)
    p = jax.nn.softmax(s, axis=-1)
    w = p[0] - lam * p[1]
    return jnp.einsum('bhqk,bkhe->bqhe', w.astype(v.dtype), v)


def diff_head_out(o, subln_g, lam_init):
    o = rmsnorm(o, subln_g) * (1.0 - lam_init)
    return o.reshape(o.shape[0], o.shape[1], W_ATT)


def _cmul(ar, ai, br, bi):
    return ar * br - ai * bi, ar * bi + ai * br


def _scan_combine(e1, e2):
    a1r, a1i, b1r, b1i = e1
    a2r, a2i, b2r, b2i = e2
    ar, ai = _cmul(a2r, a2i, a1r, a1i)
    br, bi = _cmul(a2r, a2i, b1r, b1i)
    return ar, ai, br + b2r, bi + b2i


def ssm_discretise(a_re, a_im, log_dt, b_re, b_im):
    dt = jnp.exp(log_dt)[:, None]
    mag = jnp.exp(dt * a_re)
    abar_r = mag * jnp.cos(dt * a_im)
    abar_i = mag * jnp.sin(dt * a_im)
    den = a_re * a_re + a_im * a_im
    nr = abar_r - 1.0
    coef_r = (nr * a_re + abar_i * a_im) / den
    coef_i = (abar_i * a_re - nr * a_im) / den
    bbar_r, bbar_i = _cmul(coef_r[..., None], coef_i[..., None], b_re, b_im)
    return abar_r, abar_i, bbar_r, bbar_i


def ssm_direction(u, disc, reverse, h0=None):
    abar_r, abar_i, bbar_r, bbar_i = disc
    bu_r = jnp.einsum('blgp,gnp->blgn', u, bbar_r)
    bu_i = jnp.einsum('blgp,gnp->blgn', u, bbar_i)
    a_r = jnp.broadcast_to(abar_r, bu_r.shape)
    a_i = jnp.broadcast_to(abar_i, bu_r.shape)
    acc_r, acc_i, h_r, h_i = lax.associative_scan(
        _scan_combine, (a_r, a_i, bu_r, bu_i), reverse=reverse, axis=1)
    if h0 is not None:
        cr, ci = _cmul(acc_r, acc_i, h0[0][:, None], h0[1][:, None])
        h_r = h_r + cr
        h_i = h_i + ci
    return h_r, h_i


def ssm_readout(h_r, h_i, c_re, c_im):
    return jnp.einsum('blgn,gpn->blgp', h_r, c_re) - jnp.einsum('blgn,gpn->blgp', h_i, c_im)


def s5_glu(y, w_glu, dtype):
    y = jax.nn.gelu(y.reshape(y.shape[0], y.shape[1], W_SSM).astype(dtype))
    return y * jax.nn.sigmoid(y @ w_glu)


def ssm_mixer(u_lat, u_ctx, a_re, a_im, log_dt, b_re, b_im, c_re, c_im, d_skip, w_glu, need_ctx):
    dtype = u_lat.dtype
    f32 = jnp.float32
    bsz = u_lat.shape[0]
    ul = u_lat.astype(f32).reshape(bsz, u_lat.shape[1], SSM_GROUPS, SSM_GROUP)
    uc = u_ctx.astype(f32).reshape(bsz, u_ctx.shape[1], SSM_GROUPS, SSM_GROUP)
    d = d_skip.astype(f32).reshape(SSM_GROUPS, SSM_GROUP)
    y_lat = d * ul
    y_ctx = d * uc
    for dirn, rev in ((0, False), (1, True)):
        disc = ssm_discretise(a_re[dirn].astype(f32), a_im[dirn].astype(f32), log_dt[dirn].astype(f32),
                              b_re[dirn].astype(f32), b_im[dirn].astype(f32))
        cr, ci = c_re[dirn].astype(f32), c_im[dirn].astype(f32)
        hc_r, hc_i = ssm_direction(uc, disc, rev)
        last = 0 if rev else -1
        h0 = (hc_r[:, last], hc_i[:, last])
        hl_r, hl_i = ssm_direction(ul, disc, rev, h0)
        y_lat = y_lat + ssm_readout(hl_r, hl_i, cr, ci)
        if need_ctx:
            y_ctx = y_ctx + ssm_readout(hc_r, hc_i, cr, ci)
    out_lat = s5_glu(y_lat, w_glu, dtype)
    out_ctx = s5_glu(y_ctx, w_glu, dtype) if need_ctx else None
    return out_lat, out_ctx


def mixer(h_lat, h_ctx, w_in, w_out, conv_w, lam_q1, lam_k1, lam_q2, lam_k2, subln_g,
          a_re, a_im, log_dt, b_re, b_im, c_re, c_im, d_skip, w_glu,
          rope_cos, rope_sin, lam_init, need_ctx):
    bsz, n_lat, _ = h_lat.shape
    n_ctx = h_ctx.shape[1]
    p_lat = h_lat @ w_in
    base = 0 if need_ctx else OFF_K
    p_ctx = h_ctx @ w_in[:, base:]

    def pl(off, width):
        return p_lat[..., off:off + width]

    def pc(off, width):
        return p_ctx[..., off - base:off - base + width]

    y_conv = short_conv(pl(OFF_CONV_H, W_CONV), pl(OFF_CONV_B, W_CONV), pl(OFF_CONV_C, W_CONV), conv_w)

    f32 = jnp.float32
    lam = (jnp.exp(jnp.sum(lam_q1.astype(f32) * lam_k1.astype(f32)))
           - jnp.exp(jnp.sum(lam_q2.astype(f32) * lam_k2.astype(f32))) + lam_init)

    def heads(t, length):
        return t.reshape(bsz, length, ATT_HEADS, 2, ATT_HEAD_DIM)

    q_lat = apply_axial_rope(heads(pl(OFF_Q, W_ATT), n_lat), rope_cos, rope_sin)
    k_lat = apply_axial_rope(heads(pl(OFF_K, W_ATT), n_lat), rope_cos, rope_sin)
    v_lat = pl(OFF_V, W_ATT).reshape(bsz, n_lat, ATT_HEADS, 2 * ATT_HEAD_DIM)
    k_ctx = heads(pc(OFF_K, W_ATT), n_ctx)
    v_ctx = pc(OFF_V, W_ATT).reshape(bsz, n_ctx, ATT_HEADS, 2 * ATT_HEAD_DIM)
    k_all = jnp.concatenate([k_lat, k_ctx], axis=1)
    v_all = jnp.concatenate([v_lat, v_ctx], axis=1)
    n_blk = n_lat // Q_BLOCK
    qb = q_lat.reshape(bsz, n_blk, Q_BLOCK, ATT_HEADS, 2, ATT_HEAD_DIM).transpose(1, 0, 2, 3, 4, 5)
    ob = lax.map(lambda qq: diff_attend(qq, k_all, v_all, lam), qb)
    o_lat = ob.transpose(1, 0, 2, 3, 4).reshape(bsz, n_lat, ATT_HEADS, 2 * ATT_HEAD_DIM)
    y_att = diff_head_out(o_lat, subln_g, lam_init)

    y_ssm, y_ssm_ctx = ssm_mixer(pl(OFF_U, W_SSM), pc(OFF_U, W_SSM), a_re, a_im, log_dt,
                                 b_re, b_im, c_re, c_im, d_skip, w_glu, need_ctx)

    out_lat = jnp.concatenate([y_conv, y_att, y_ssm], axis=-1) @ w_out
    if not need_ctx:
        return out_lat, None

    y_conv_c = short_conv(pc(OFF_CONV_H, W_CONV), pc(OFF_CONV_B, W_CONV), pc(OFF_CONV_C, W_CONV), conv_w)
    q_ctx = heads(pc(OFF_Q, W_ATT), n_ctx)
    y_att_c = diff_head_out(diff_attend(q_ctx, k_ctx, v_ctx, lam), subln_g, lam_init)
    out_ctx = jnp.concatenate([y_conv_c, y_att_c, y_ssm_ctx], axis=-1) @ w_out
    return out_lat, out_ctx


def setup_inputs(seed: int = 0) -> dict:
    key = jax.random.key(seed)
    ks = iter(jax.random.split(key, 40))
    f32 = jnp.float32

    def nrm(shape, scale):
        return scale * jax.random.normal(next(ks), shape, f32)

    D, L = D_MODEL, DEPTH
    G, N, P = SSM_GROUPS, SSM_STATE, SSM_GROUP
    n_idx = jnp.arange(N, dtype=f32)
    return {
        "x": nrm((BATCH, SEQ, D), 1.0),
        "c": nrm((BATCH, D), 1.0),
        "ctx": nrm((BATCH, CTX_LEN, D), 1.0),
        "c_ctx": nrm((D,), 1.0),
        "w_ada": nrm((L, D, N_MOD * D), 0.5 * D ** -0.5),
        "b_ada": nrm((L, N_MOD * D), 0.02),
        "ffa_w1": nrm((L, D, D_FF), D ** -0.5),
        "ffa_w3": nrm((L, D, D_FF), D ** -0.5),
        "ffa_w2": nrm((L, D_FF, D), D_FF ** -0.5),
        "ffb_w1": nrm((L, D, D_FF), D ** -0.5),
        "ffb_w3": nrm((L, D, D_FF), D ** -0.5),
        "ffb_w2": nrm((L, D_FF, D), D_FF ** -0.5),
        "w_in": nrm((L, D, D_IN_PROJ), D ** -0.5),
        "w_out": nrm((L, D_MIX, D), D_MIX ** -0.5),
        "conv_w": nrm((L, 3, W_CONV), 3 ** -0.5),
        "lam_q1": nrm((L, ATT_HEAD_DIM), 0.1),
        "lam_k1": nrm((L, ATT_HEAD_DIM), 0.1),
        "lam_q2": nrm((L, ATT_HEAD_DIM), 0.1),
        "lam_k2": nrm((L, ATT_HEAD_DIM), 0.1),
        "subln_g": 1.0 + nrm((L, 2 * ATT_HEAD_DIM), 0.02),
        "ssm_a_re": -0.5 * jnp.exp(nrm((L, 2, G, N), 0.05)),
        "ssm_a_im": math.pi * n_idx + nrm((L, 2, G, N), 0.01),
        "ssm_log_dt": jax.random.uniform(next(ks), (L, 2, G), f32, math.log(1e-3), math.log(1e-1)),
        "ssm_b_re": nrm((L, 2, G, N, P), (2 * P) ** -0.5),
        "ssm_b_im": nrm((L, 2, G, N, P), (2 * P) ** -0.5),
        "ssm_c_re": nrm((L, 2, G, P, N), (2 * N) ** -0.5),
        "ssm_c_im": nrm((L, 2, G, P, N), (2 * N) ** -0.5),
        "ssm_d": nrm((L, W_SSM), 1.0),
        "ssm_w_glu": nrm((L, W_SSM, W_SSM), W_SSM ** -0.5),
        "final_g": 1.0 + nrm((D,), 0.02),
    }


def reference(x, c, ctx, c_ctx, w_ada, b_ada, ffa_w1, ffa_w3, ffa_w2, ffb_w1, ffb_w3, ffb_w2,
              w_in, w_out, conv_w, lam_q1, lam_k1, lam_q2, lam_k2, subln_g,
              ssm_a_re, ssm_a_im, ssm_log_dt, ssm_b_re, ssm_b_im, ssm_c_re, ssm_c_im,
              ssm_d, ssm_w_glu, final_g):
    ROWS = x.shape[1] // GRID_W
    rope_cos, rope_sin = axial_rope_tables(ROWS)
    xc = ctx
    for l in range(DEPTH):
        need_ctx = l < DEPTH - 1
        lam_init = 0.8 - 0.6 * math.exp(-0.3 * l)
        mod_lat = jax.nn.silu(c) @ w_ada[l] + b_ada[l]
        mod_ctx = jax.nn.silu(c_ctx) @ w_ada[l] + b_ada[l]
        s1, sc1, g1, s2, sc2, g2, s3, sc3, g3 = jnp.split(mod_lat[:, None, :], N_MOD, axis=-1)
        t1, tc1, h1, t2, tc2, h2, t3, tc3, h3 = jnp.split(mod_ctx, N_MOD, axis=-1)

        x = x + 0.5 * g1 * swiglu(modulate(x, s1, sc1), ffa_w1[l], ffa_w3[l], ffa_w2[l])
        xc = xc + 0.5 * h1 * swiglu(modulate(xc, t1, tc1), ffa_w1[l], ffa_w3[l], ffa_w2[l])

        o_lat, o_ctx = mixer(modulate(x, s2, sc2), modulate(xc, t2, tc2), w_in[l], w_out[l], conv_w[l],
                             lam_q1[l], lam_k1[l], lam_q2[l], lam_k2[l], subln_g[l],
                             ssm_a_re[l], ssm_a_im[l], ssm_log_dt[l], ssm_b_re[l], ssm_b_im[l],
                             ssm_c_re[l], ssm_c_im[l], ssm_d[l], ssm_w_glu[l],
                             rope_cos, rope_sin, lam_init, need_ctx)
        x = x + g2 * o_lat

        x = x + 0.5 * g3 * swiglu(modulate(x, s3, sc3), ffb_w1[l], ffb_w3[l], ffb_w2[l])
        if need_ctx:
            xc = xc + h2 * o_ctx
            xc = xc + 0.5 * h3 * swiglu(modulate(xc, t3, tc3), ffb_w1[l], ffb_w3[l], ffb_w2[l])
    return rmsnorm(x, final_g)
```

```python
import math, contextlib
import numpy as np
import concourse.bass as bass
import concourse.mybir as mybir
from concourse.bass_utils import run_bass_kernel_spmd

F32 = mybir.dt.float32; BF16 = mybir.dt.bfloat16; I32 = mybir.dt.int32
ALU = mybir.AluOpType; AF = mybir.ActivationFunctionType; AX = mybir.AxisListType
COMPUTE = ("pe", "act", "dve")
DMAQ = ("sync", "pool")
NS = 8


class Cfg:
    def __init__(s, D=4096, SEQ=8192, CTX=256, DEPTH=2, B=2):
        s.D, s.SEQ, s.CTX, s.DEPTH, s.B = D, SEQ, CTX, DEPTH, B
        s.KT = D // 128; s.WC = D // 4; s.WA = D // 2; s.WS = D - s.WC - s.WA
        s.H = s.WA // 128; s.G = s.WS // 16; s.GT = s.WS // 128
        s.DFF = 2 * D; s.MT = s.DFF // 128
        s.DIN = 3 * s.WC + 3 * s.WA + s.WS
        s.NLAT = B * SEQ; s.NCTX = B * CTX; s.NTOK = s.NLAT + s.NCTX
        s.TG = 512; s.NG = s.NTOK // 512; s.NGL = s.NLAT // 512
        s.OFF_B = s.WC; s.OFF_C = 2 * s.WC; s.OFF_Q = 3 * s.WC; s.OFF_K = s.OFF_Q + s.WA
        s.OFF_V = s.OFF_K + s.WA; s.OFF_U = s.OFF_V + s.WA
        s.fm_cols = ([i * 128 for i in range(3 * s.WC // 128)] + [s.OFF_Q + i * 128 for i in range(s.WA // 128)]
                     + [s.OFF_K + i * 128 for i in range(s.WA // 128)] + [s.OFF_U + i * 128 for i in range(s.WS // 128)])
        s.NFM = len(s.fm_cols)
        s.CT = s.WC // 128
        s.LS = CTX + SEQ
        assert s.NCTX == 512 and SEQ % 512 == 0


class Prog:
    def __init__(s, nc, stack):
        s.nc = nc
        s.sems = []
        def mk(n):
            s.sems.append(stack.enter_context(nc.semaphore(n))); return len(s.sems) - 1
        s.esem = {e: mk("s_" + e) for e in COMPUTE}
        s.dsem = {q: [mk("d_%s%d" % (q, i)) for i in range(NS)] for q in DMAQ}
        s.q = {e: [] for e in COMPUTE + DMAQ}
        s.cnt = {e: 0 for e in COMPUTE}
        s.dcnt = {q: 0 for q in DMAQ}
        s.lastw = {}; s.reads = {}
        s.seen = {e: {} for e in COMPUTE + DMAQ}

    def _deps(s, eng, r, w):
        need = {}
        def add(d):
            for sem, (val, e) in d.items():
                if e == eng and eng == "pe": continue
                if s.seen[eng].get(sem, 0) >= val: continue
                if need.get(sem, 0) < val: need[sem] = val
        for k in r:
            if k in s.lastw: add(s.lastw[k])
        for k in w:
            if k in s.lastw: add(s.lastw[k])
            if k in s.reads: add(s.reads[k])
        for sem, val in need.items(): s.seen[eng][sem] = val
        return list(need.items())

    def _commit(s, ev, r, w):
        sem, val, e = ev
        for k in r:
            s.reads.setdefault(k, {})[sem] = (val, e)
        for k in w:
            s.lastw[k] = {sem: (val, e)}; s.reads[k] = {}

    def op(s, eng, fn, r=(), w=()):
        waits = s._deps(eng, r, w)
        s.cnt[eng] += 1
        s.q[eng].append((waits, fn, s.esem[eng], 1))
        s._commit((s.esem[eng], s.cnt[eng], eng), r, w)

    def dma(s, q, out, in_, r=(), w=()):
        i = s.dcnt[q]; s.dcnt[q] += 1
        sem = s.dsem[q][i % NS]; val = 16 * (i // NS + 1)
        waits = s._deps(q, r, w)
        if i >= NS and s.seen[q].get(sem, 0) < val - 16:
            waits = [(a, b) for (a, b) in waits if a != sem] + [(sem, max(val - 16, dict(waits).get(sem, 0)))]
            s.seen[q][sem] = max(val - 16, s.seen[q].get(sem, 0))
        s.q[q].append((waits, (lambda e, o=out, i_=in_: e.dma_start(out=o, in_=i_, allow_slow_non_contiguous=True)), sem, 16))
        s._commit((sem, val, q), r, w)

    def barrier(s):
        allv = {}
        for e in COMPUTE: allv[s.esem[e]] = s.cnt[e]
        for q in DMAQ:
            for j in range(NS):
                n = (s.dcnt[q] - j + NS - 1) // NS if s.dcnt[q] > j else 0
                allv[s.dsem[q][j]] = 16 * n
        for e in COMPUTE + DMAQ:
            waits = []
            for sem, val in allv.items():
                if val > 0 and s.seen[e].get(sem, 0) < val and not (e in COMPUTE and sem == s.esem[e]):
                    waits.append((sem, val)); s.seen[e][sem] = val
            if waits: s.q[e].append((waits, None, None, 0))
        s.lastw = {}; s.reads = {}

    def emit(s):
        nc = s.nc
        s.barrier()
        names = {"pe": "tensor", "act": "scalar", "dve": "vector", "sync": "sync", "pool": "gpsimd"}
        with nc.Block() as block:
            for e in COMPUTE + DMAQ:
                def body(eng, lst=s.q[e]):
                    for waits, fn, sem, inc in lst:
                        for ws, wv in waits: eng.wait_ge(s.sems[ws], wv)
                        if fn is not None:
                            fn(eng).then_inc(s.sems[sem], inc)
                getattr(block, names[e])(body)


DEBUG = False
MIXER_PARTS = 7
SSM_DIRS = (0, 1)
SSM_NSEG = 4


def build(cfg, nlayers=None):
    c = cfg; D, KT, MT = c.D, c.KT, c.MT
    L = c.DEPTH if nlayers is None else nlayers
    nc = bass.Bass("TRN2", target_bir_lowering=False)
    def din(name, shape, dt=F32): return nc.dram_tensor(name, list(shape), dt, kind="ExternalInput").ap()
    def dsc(name, shape, dt): return nc.dram_tensor(name, list(shape), dt).ap()
    G, N, P = c.G, 64, 16
    I = {}
    I["x"] = din("x", [c.NLAT, D]); I["c"] = din("c", [c.B, D]); I["ctx"] = din("ctx", [c.NCTX, D]); I["c_ctx"] = din("c_ctx", [1, D])
    I["w_ada"] = din("w_ada", [c.DEPTH, D, 9 * D]); I["b_ada"] = din("b_ada", [c.DEPTH, 9 * D])
    for n_ in ("ffa_w1", "ffa_w3", "ffb_w1", "ffb_w3"): I[n_] = din(n_, [c.DEPTH, D, c.DFF])
    for n_ in ("ffa_w2", "ffb_w2"): I[n_] = din(n_, [c.DEPTH, c.DFF, D])
    I["w_in"] = din("w_in", [c.DEPTH, D, c.DIN]); I["w_out"] = din("w_out", [c.DEPTH, D, D])
    I["conv_w"] = din("conv_w", [c.DEPTH, 3, c.WC])
    for n_ in ("lam_q1", "lam_k1", "lam_q2", "lam_k2"): I[n_] = din(n_, [c.DEPTH, 64])
    I["subln_g"] = din("subln_g", [c.DEPTH, 128])
    I["ssm_a_re"] = din("ssm_a_re", [c.DEPTH, 2 * G * N]); I["ssm_a_im"] = din("ssm_a_im", [c.DEPTH, 2 * G * N])
    I["ssm_log_dt"] = din("ssm_log_dt", [c.DEPTH, 2 * G])
    I["ssm_b_re"] = din("ssm_b_re", [c.DEPTH, 2 * G * N, P]); I["ssm_b_im"] = din("ssm_b_im", [c.DEPTH, 2 * G * N, P])
    I["ssm_c_re"] = din("ssm_c_re", [c.DEPTH, 2 * G, P, N]); I["ssm_c_im"] = din("ssm_c_im", [c.DEPTH, 2 * G, P, N])
    I["ssm_d"] = din("ssm_d", [c.DEPTH, c.WS]); I["ssm_w_glu"] = din("ssm_w_glu", [c.DEPTH, c.WS, c.WS])
    I["final_g"] = din("final_g", [1, D])
    I["ident"] = din("ident", [128, 128]); I["rotm"] = din("rotm", [128, 128])
    I["ropec"] = din("ropec", [128, c.SEQ]); I["ropes"] = din("ropes", [128, c.SEQ])
    OUT = nc.dram_tensor("out", [c.NLAT, D], F32, kind="ExternalOutput").ap()
    if DEBUG:
        DBM = nc.dram_tensor("dbg_mod", [3, 9 * D], F32, kind="ExternalOutput").ap()
        DBX = nc.dram_tensor("dbg_x", [c.NTOK, D], F32, kind="ExternalOutput").ap()
        DBY = nc.dram_tensor("dbg_y", [c.KT, 128, c.NTOK], F32, kind="ExternalOutput").ap()
        DBP = nc.dram_tensor("dbg_p", [6, 128, c.G], F32, kind="ExternalOutput").ap()
        DBG_ = nc.dram_tensor("dbg_yg", [c.GT, 128, c.NTOK], F32, kind="ExternalOutput").ap()
    XP = [dsc("X%d" % b, [c.SEQ, D], F32) for b in range(c.B)] + [dsc("XC", [c.NCTX, D], F32)]
    def XS(t0, t1):
        if t0 >= c.NLAT: return XP[c.B][t0 - c.NLAT:t1 - c.NLAT, :]
        b = t0 // c.SEQ; assert (t1 - 1) // c.SEQ == b
        return XP[b][t0 - b * c.SEQ:t1 - b * c.SEQ, :]
    MOD = dsc("MOD", [L, 3, 9 * D], F32)
    W1t = {}; W3t = {}; W2t = {}
    for l in range(L):
        for ab in "ab":
            W1t[l, ab] = dsc("W1t%d%s" % (l, ab), [MT, 128, KT, 128], BF16)
            W3t[l, ab] = dsc("W3t%d%s" % (l, ab), [MT, 128, KT, 128], BF16)
            W2t[l, ab] = dsc("W2t%d%s" % (l, ab), [D // 512, 128, MT, 512], BF16)
    Wfm = [dsc("Wfm%d" % l, [c.NFM, 128, KT, 128], BF16) for l in range(L)]
    Wv = [dsc("Wv%d" % l, [c.WA // 512, 128, KT, 512], BF16) for l in range(L)]
    Wo = [dsc("Wo%d" % l, [D // 512, 128, KT, 512], BF16) for l in range(L)]
    Wg = [dsc("Wg%d" % l, [c.GT, 128, c.GT, 128], BF16) for l in range(L)]
    CVS = dsc("CVS", [3 * c.CT, 128, c.NTOK], F32)
    QT = dsc("QT", [c.H, 128, c.NTOK], BF16); KTd = dsc("KTd", [c.H, 128, c.NTOK], BF16)
    UT = dsc("UT", [c.GT, 128, c.NTOK], BF16); Vd = dsc("Vd", [c.NTOK, c.WA], BF16)
    YT = dsc("YT", [KT, 128, c.NTOK], BF16)
    YG = dsc("YG", [c.GT, 128, c.NTOK], BF16)
    LAMD = dsc("LAMD", [1, 2], F32)

    with contextlib.ExitStack() as stack:
        pg = Prog(nc, stack)
        uid = [0]
        def sb(name, shape, dt=F32):
            uid[0] += 1; return stack2.enter_context(nc.sbuf_tensor("sb%d_%s" % (uid[0], name), list(shape), dt))
        def ps(name, shape, dt=F32):
            uid[0] += 1; return stack2.enter_context(nc.psum_tensor("ps%d_%s" % (uid[0], name), list(shape), dt))
        op, dma = pg.op, pg.dma

        def castw(dst, src):
            dma("pool", dst, src, w=[("wcast",)])
        for l in range(L):
            for ab, (a1, a3, a2) in (("a", ("ffa_w1", "ffa_w3", "ffa_w2")), ("b", ("ffb_w1", "ffb_w3", "ffb_w2"))):
                v1 = I[a1][l].rearrange("(kt p) n -> p kt n", p=128); v3 = I[a3][l].rearrange("(kt p) n -> p kt n", p=128)
                v2 = I[a2][l].rearrange("(mt p) n -> p mt n", p=128)
                for mt in range(MT):
                    castw(W1t[l, ab][mt], v1[:, :, mt * 128:(mt + 1) * 128]); castw(W3t[l, ab][mt], v3[:, :, mt * 128:(mt + 1) * 128])
                for n_ in range(D // 512):
                    for hf in range(2):
                        castw(W2t[l, ab][n_][:, hf * MT // 2:(hf + 1) * MT // 2, :], v2[:, hf * MT // 2:(hf + 1) * MT // 2, n_ * 512:(n_ + 1) * 512])
            vi = I["w_in"][l].rearrange("(kt p) n -> p kt n", p=128)
            for i, c0 in enumerate(c.fm_cols): castw(Wfm[l][i], vi[:, :, c0:c0 + 128])
            for i in range(c.WA // 512): castw(Wv[l][i], vi[:, :, c.OFF_V + i * 512:c.OFF_V + (i + 1) * 512])
            vo = I["w_out"][l].rearrange("(kt p) n -> p kt n", p=128)
            for i in range(D // 512): castw(Wo[l][i], vo[:, :, i * 512:(i + 1) * 512])
            vg = I["ssm_w_glu"][l].rearrange("(kt p) n -> p kt n", p=128)
            for i in range(c.GT): castw(Wg[l][i], vg[:, :, i * 128:(i + 1) * 128])
        for i in range(c.NLAT // 1024): dma("sync", XS(i * 1024, (i + 1) * 1024), I["x"][i * 1024:(i + 1) * 1024, :], w=[("wcast",)])
        dma("sync", XS(c.NLAT, c.NTOK), I["ctx"], w=[("wcast",)])
        pg.barrier()

        def grow(g):
            return 2 if g >= c.NGL else g // (c.SEQ // 512)

        cst = contextlib.ExitStack(); stack2 = cst
        ident = sb("ident", [128, 128]); rotm = sb("rotm", [128, 128]); ones_b = sb("ones_b", [128, 128], BF16); ones_f = sb("ones_f", [128, 128])
        modp = sb("modp", [128, 3, 9, KT])
        dma("sync", ident[:], I["ident"], w=["ident"]); dma("sync", rotm[:], I["rotm"], w=["rotm"])
        op("dve", lambda e: e.memset(ones_b[:], 1.0), w=["ones_b"]); op("dve", lambda e: e.memset(ones_f[:], 1.0), w=["ones_f"])

        def phase_mod():
            nonlocal stack2
            with contextlib.ExitStack() as st:
                stack2 = st
                ct = sb("ct", [128, KT, 3]); sct = sb("sct", [128, KT, 3])
                wa = [sb("wa%d" % i, [128, KT // 2, 512]) for i in range(2)]
                bb = sb("bb", [3, 512]); mo = [sb("mo%d" % i, [3, 512]) for i in range(2)]
                pm = [ps("pm%d" % i, [3, 512]) for i in range(2)]
                for r in range(3):
                    src = (I["c"][r] if r < 2 else I["c_ctx"][0]).rearrange("(kt p) -> p kt", p=128)
                    dma("sync", ct[:, :, r], src, w=[("ct", r)])
                op("act", lambda e: e.activation(out=sct[:], in_=ct[:], func=AF.Silu), r=[("ct", 0), ("ct", 1), ("ct", 2)], w=["sct"])
                k = 0
                for l in range(L):
                    wv = I["w_ada"][l].rearrange("(kt p) n -> p kt n", p=128)
                    for j in range(9 * D // 512):
                        pj = pm[j % 2]
                        for hf in range(2):
                            wt = wa[k % 2]; k += 1
                            dma("sync", wt[:], wv[:, hf * KT // 2:(hf + 1) * KT // 2, j * 512:(j + 1) * 512], w=[("wa", id(wt))])
                            for kk in range(KT // 2):
                                kt = hf * KT // 2 + kk
                                op("pe", lambda e, pj=pj, wt=wt, kk=kk, kt=kt: e.matmul(pj[:], lhsT=sct[:, kt, :], rhs=wt[:, kk, :], start=(kt == 0), stop=(kt == KT - 1)),
                                   r=["sct", ("wa", id(wt))], w=[("pm", id(pj))])
                        dma("sync", bb[:], I["b_ada"][l:l + 1, j * 512:(j + 1) * 512].to_broadcast([3, 512]), w=["bb"])
                        m = mo[j % 2]
                        op("dve", lambda e, m=m, pj=pj: e.tensor_tensor(out=m[:], in0=pj[:], in1=bb[:], op=ALU.add), r=[("pm", id(pj)), "bb"], w=[("mo", id(m))])
                        dma("pool", MOD[l][:, j * 512:(j + 1) * 512], m[:], r=[("mo", id(m))], w=[("MOD", l, j)])
            pg.barrier()

        def load_modp(l):
            for r in range(3):
                dma("sync", modp[:, r], MOD[l][r].rearrange("(m kt p) -> p m kt", p=128, kt=KT), w=[("modp", r)])
            for m in (1, 4, 7):
                op("dve", lambda e, m=m: e.tensor_scalar(out=modp[:, :, m, :], in0=modp[:, :, m, :], scalar1=1.0, scalar2=None, op0=ALU.add),
                   r=[("modp", 0), ("modp", 1), ("modp", 2)], w=[("modp", 0), ("modp", 1), ("modp", 2)])

        def norm_mod(g, mslot, xt, junk, ss, xmT, ptr, gain=None):
            row = grow(g)
            for tt in range(4):
                t0 = g * 512 + tt * 128
                dma("sync", xt[:], XS(t0, t0 + 128), r=[("X", g)], w=["xt"])
                op("dve", lambda e: e.tensor_tensor(out=junk[:], in0=xt[:], in1=xt[:], op=ALU.mult), r=["xt"], w=["junk"])
                op("dve", lambda e: e.tensor_reduce(out=ss[:, 0:1], in_=junk[:], axis=AX.X, op=ALU.add), r=["junk"], w=["ss"])
                op("act", lambda e: e.activation(out=ss[:, 1:2], in_=ss[:, 0:1], func=AF.Sqrt, scale=1.0 / D, bias=1e-6), r=["ss"], w=["ss"])
                op("dve", lambda e: e.reciprocal(out=ss[:, 2:3], in_=ss[:, 1:2]), r=["ss"], w=["ss"])
                op("dve", lambda e: e.tensor_scalar(out=xt[:], in0=xt[:], scalar1=ss[:, 2:3], scalar2=None, op0=ALU.mult), r=["ss", "xt"], w=["xt"])
                for k4 in range(KT // 4):
                    pt_ = ptr[k4 % 2]
                    for kk in range(4):
                        kt = k4 * 4 + kk
                        op("pe", lambda e, pt_=pt_, kk=kk, kt=kt: e.transpose(out=pt_[:, kk * 128:(kk + 1) * 128], in_=xt[:, kt * 128:(kt + 1) * 128], identity=ident[:]),
                           r=["xt", "ident"], w=[("ptr", id(pt_))])
                    for kk in range(4):
                        kt = k4 * 4 + kk
                        op("act", lambda e, pt_=pt_, kk=kk, kt=kt, tt=tt: e.activation(out=xmT[:, kt, tt * 128:(tt + 1) * 128], in_=pt_[:, kk * 128:(kk + 1) * 128], func=AF.Identity,
                                                                             scale=modp[:, row, mslot + 1, kt:kt + 1], bias=modp[:, row, mslot, kt:kt + 1]),
                           r=[("ptr", id(pt_)), ("modp", row)], w=["xmT"])

        def phase_ffn(l, ab, ngroups):
            nonlocal stack2
            mslot = 0 if ab == "a" else 6
            with contextlib.ExitStack() as st:
                stack2 = st
                xt = sb("xt", [128, D]); junk = sb("junk", [128, D]); ss = sb("ss", [128, 4])
                xmT = sb("xmT", [128, KT, 512], BF16); hT = sb("hT", [128, MT, 512], BF16)
                w1 = [sb("w1_%d" % i, [128, KT, 128], BF16) for i in range(2)]; w3 = [sb("w3_%d" % i, [128, KT, 128], BF16) for i in range(2)]
                w2 = [sb("w2_%d" % i, [128, 8, 512], BF16) for i in range(2)]
                sl = [sb("sl%d" % i, [128, 512]) for i in range(2)]
                gb = sb("gb", [128, 512]); xc = [sb("xc%d" % i, [128, 512]) for i in range(2)]; tm = [sb("tm%d" % i, [128, 512]) for i in range(2)]
                pA = [ps("pA%d" % i, [128, 512]) for i in range(4)]; pO = [ps("pO%d" % i, [128, 512]) for i in range(4)]
                for g in range(ngroups):
                    row = grow(g)
                    norm_mod(g, mslot, xt, junk, ss, xmT, pA[0:2])
                    for mt in range(MT):
                        a1, a3 = w1[mt % 2], w3[mt % 2]; p1, p3 = pA[(mt % 2) * 2], pA[(mt % 2) * 2 + 1]
                        dma("sync", a1[:], W1t[l, ab][mt], w=[("w1", mt % 2)]); dma("sync", a3[:], W3t[l, ab][mt], w=[("w3", mt % 2)])
                        for kt in range(KT):
                            op("pe", lambda e, p1=p1, a1=a1, kt=kt: e.matmul(p1[:], lhsT=a1[:, kt, :], rhs=xmT[:, kt, :], start=(kt == 0), stop=(kt == KT - 1)),
                               r=[("w1", mt % 2), "xmT"], w=[("ptr", id(p1))])
                        for kt in range(KT):
                            op("pe", lambda e, p3=p3, a3=a3, kt=kt: e.matmul(p3[:], lhsT=a3[:, kt, :], rhs=xmT[:, kt, :], start=(kt == 0), stop=(kt == KT - 1)),
                               r=[("w3", mt % 2), "xmT"], w=[("ptr", id(p3))])
                        s_ = sl[mt % 2]
                        op("act", lambda e, s_=s_, p1=p1: e.activation(out=s_[:], in_=p1[:], func=AF.Silu), r=[("ptr", id(p1))], w=[("sl", mt % 2)])
                        op("dve", lambda e, s_=s_, p3=p3, mt=mt: e.tensor_tensor(out=hT[:, mt, :], in0=s_[:], in1=p3[:], op=ALU.mult),
                           r=[("sl", mt % 2), ("ptr", id(p3))], w=[("hT", mt)])
                    k = 0
                    for n_ in range(D // 512):
                        dma("sync", gb[:], MOD[l][row:row + 1, (mslot + 2) * D + n_ * 512:(mslot + 2) * D + (n_ + 1) * 512].to_broadcast([128, 512]), w=["gb"])
                        for q8 in range(MT // 8):
                            wt = w2[k % 2]; wk = ("w2", k % 2); k += 1
                            dma("sync", wt[:], W2t[l, ab][n_][:, q8 * 8:(q8 + 1) * 8, :], w=[wk])
                            for ml in range(8):
                                mt = q8 * 8 + ml
                                for tt in range(4):
                                    op("pe", lambda e, wt=wt, ml=ml, mt=mt, tt=tt: e.matmul(pO[tt][:], lhsT=hT[:, mt, tt * 128:(tt + 1) * 128], rhs=wt[:, ml, :], start=(mt == 0), stop=(mt == MT - 1)),
                                       r=[wk, ("hT", mt)], w=[("pO", tt)])
                        for tt in range(4):
                            t0 = g * 512 + tt * 128; xcc = xc[tt % 2]; tmm = tm[tt % 2]
                            dma("sync", xcc[:], XS(t0, t0 + 128)[:, n_ * 512:(n_ + 1) * 512], r=[("X", g)], w=[("xc", tt % 2)])
                            op("dve", lambda e, tmm=tmm, tt=tt: e.scalar_tensor_tensor(out=tmm[:], in0=pO[tt][:], scalar=0.5, in1=gb[:], op0=ALU.mult, op1=ALU.mult),
                               r=[("pO", tt), "gb"], w=[("tm", tt % 2)])
                            op("dve", lambda e, tmm=tmm, xcc=xcc: e.tensor_tensor(out=tmm[:], in0=tmm[:], in1=xcc[:], op=ALU.add), r=[("tm", tt % 2), ("xc", tt % 2)], w=[("tm", tt % 2)])
                            dma("pool", XS(t0, t0 + 128)[:, n_ * 512:(n_ + 1) * 512], tmm[:], r=[("tm", tt % 2)], w=[("Xw", g, n_, tt)])
            pg.barrier()

        def phase_final():
            nonlocal stack2
            with contextlib.ExitStack() as st:
                stack2 = st
                xt = [sb("fx%d" % i, [128, D]) for i in range(2)]; junk = sb("fjunk", [128, D]); ss = sb("fss", [128, 4]); fg = sb("fg", [128, D])
                dma("sync", fg[:], I["final_g"][0:1, :].to_broadcast([128, D]), w=["fg"])
                for i in range(c.NLAT // 128):
                    x_ = xt[i % 2]; kx = ("fx", i % 2)
                    dma("sync", x_[:], XS(i * 128, (i + 1) * 128), w=[kx])
                    op("dve", lambda e, x_=x_: e.tensor_tensor(out=junk[:], in0=x_[:], in1=x_[:], op=ALU.mult), r=[kx], w=["junk"])
                    op("dve", lambda e: e.tensor_reduce(out=ss[:, 0:1], in_=junk[:], axis=AX.X, op=ALU.add), r=["junk"], w=["ss"])
                    op("act", lambda e: e.activation(out=ss[:, 1:2], in_=ss[:, 0:1], func=AF.Sqrt, scale=1.0 / D, bias=1e-6), r=["ss"], w=["ss"])
                    op("dve", lambda e: e.reciprocal(out=ss[:, 2:3], in_=ss[:, 1:2]), r=["ss"], w=["ss"])
                    op("dve", lambda e, x_=x_: e.scalar_tensor_tensor(out=x_[:], in0=x_[:], scalar=ss[:, 2:3], in1=fg[:], op0=ALU.mult, op1=ALU.mult), r=["ss", kx, "fg"], w=[kx])
                    dma("pool", OUT[i * 128:(i + 1) * 128, :], x_[:], r=[kx], w=[("OUT", i)])
            pg.barrier()


        def set_stack(st):
            nonlocal stack2
            stack2 = st
        TWO_PI = 2.0 * math.pi

        def phase_inproj(l, last):
            with contextlib.ExitStack() as st:
                set_stack(st)
                xt = sb("xt", [128, D]); junk = sb("junk", [128, D]); ss = sb("ss", [128, 4]); xmT = sb("xmT", [128, KT, 512], BF16)
                wf = [sb("wf%d" % i, [128, KT, 128], BF16) for i in range(2)]; wvv = [sb("wv%d" % i, [128, KT, 512], BF16) for i in range(2)]
                ev = [sb("ev%d" % i, [128, 512]) for i in range(2)]; t1 = [sb("t1%d" % i, [128, 512]) for i in range(2)]; t2 = [sb("t2%d" % i, [128, 512]) for i in range(2)]
                ob = [sb("ob%d" % i, [128, 512], BF16) for i in range(2)]; cf = [sb("cf%d" % i, [128, 512]) for i in range(2)]
                rc = sb("rc", [128, 512]); rs = sb("rs", [128, 512])
                pF = [ps("pF%d" % i, [128, 512]) for i in range(2)]; pR = [ps("pR%d" % i, [128, 512]) for i in range(2)]
                pT = [ps("pT%d" % i, [128, 512]) for i in range(2)]; pV = [ps("pV%d" % i, [128, 512]) for i in range(2)]
                nq0 = 3 * c.CT; nk0 = nq0 + c.H; nu0 = nk0 + c.H
                cnt = 0
                for g in range(c.NG):
                    lat = g < c.NGL
                    norm_mod(g, 3, xt, junk, ss, xmT, pT)
                    cols = slice(g * 512, (g + 1) * 512)
                    if lat:
                        p0 = (g % (c.SEQ // 512)) * 512
                        dma("sync", rc[:], I["ropec"][:, p0:p0 + 512], w=["rc"]); dma("sync", rs[:], I["ropes"][:, p0:p0 + 512], w=["rs"])
                    for i in range(c.NFM):
                        if (not lat) and last and i < nk0: continue
                        w_ = wf[i % 2]; wk = ("wf", i % 2); pf = pF[i % 2]; pk = ("pF", i % 2)
                        dma("sync", w_[:], Wfm[l][i], w=[wk])
                        for kt in range(KT):
                            op("pe", lambda e, pf=pf, w_=w_, kt=kt: e.matmul(pf[:], lhsT=w_[:, kt, :], rhs=xmT[:, kt, :], start=(kt == 0), stop=(kt == KT - 1)), r=[wk, "xmT"], w=[pk])
                        j = cnt % 2; cnt += 1
                        if i < nq0:
                            op("act", lambda e, j=j, pf=pf: e.activation(out=cf[j][:], in_=pf[:], func=AF.Identity), r=[pk], w=[("cf", j)])
                            dma("pool", CVS[i][:, cols], cf[j][:], r=[("cf", j)], w=[("CVS", i, g)])
                        elif i < nu0 and lat:
                            hh = (i - nq0) if i < nk0 else (i - nk0)
                            dst = QT[hh] if i < nk0 else KTd[hh]
                            op("act", lambda e, j=j, pf=pf: e.activation(out=ev[j][:], in_=pf[:], func=AF.Identity), r=[pk], w=[("ev", j)])
                            op("pe", lambda e, j=j: e.matmul(pR[j][:], lhsT=rotm[:], rhs=ev[j][:], start=True, stop=True), r=[("ev", j), "rotm"], w=[("pR", j)])
                            op("dve", lambda e, j=j: e.tensor_tensor(out=t1[j][:], in0=ev[j][:], in1=rc[:], op=ALU.mult), r=[("ev", j), "rc"], w=[("t1", j)])
                            op("dve", lambda e, j=j: e.tensor_tensor(out=t2[j][:], in0=pR[j][:], in1=rs[:], op=ALU.mult), r=[("pR", j), "rs"], w=[("t2", j)])
                            op("dve", lambda e, j=j: e.tensor_tensor(out=ob[j][:], in0=t1[j][:], in1=t2[j][:], op=ALU.add), r=[("t1", j), ("t2", j)], w=[("ob", j)])
                            dma("pool", dst[:, cols], ob[j][:], r=[("ob", j)], w=[("QK", i, g)])
                        else:
                            if i < nk0: dst = QT[i - nq0]
                            elif i < nu0: dst = KTd[i - nk0]
                            else: dst = UT[i - nu0]
                            op("act", lambda e, j=j, pf=pf: e.activation(out=ob[j][:], in_=pf[:], func=AF.Identity), r=[pk], w=[("ob", j)])
                            dma("pool", dst[:, cols], ob[j][:], r=[("ob", j)], w=[("QK", i, g)])
                    for vc in range(c.WA // 512):
                        w_ = wvv[vc % 2]; wk = ("wv", vc % 2)
                        dma("sync", w_[:], Wv[l][vc], w=[wk])
                        for tt in range(4):
                            j = cnt % 2; cnt += 1; pv = pV[j]
                            for kt in range(KT):
                                op("pe", lambda e, pv=pv, w_=w_, kt=kt, tt=tt: e.matmul(pv[:], lhsT=xmT[:, kt, tt * 128:(tt + 1) * 128], rhs=w_[:, kt, :], start=(kt == 0), stop=(kt == KT - 1)),
                                   r=[wk, "xmT"], w=[("pV", j)])
                            op("act", lambda e, j=j, pv=pv: e.activation(out=ob[j][:], in_=pv[:], func=AF.Identity), r=[("pV", j)], w=[("ob", j)])
                            t0 = g * 512 + tt * 128
                            dma("pool", Vd[t0:t0 + 128, vc * 512:(vc + 1) * 512], ob[j][:], r=[("ob", j)], w=[("Vd", g, vc, tt)])
            pg.barrier()

        def seqs(last):
            r = [(b * c.SEQ, c.SEQ) for b in range(c.B)]
            if not last: r += [(c.NLAT + b * c.CTX, c.CTX) for b in range(c.B)]
            return r

        def phase_conv(l, last):
            CH = 2048
            with contextlib.ExitStack() as st:
                set_stack(st)
                cw = sb("cw", [128, c.CT, 3])
                for k_ in range(3): dma("sync", cw[:, :, k_], I["conv_w"][l][k_].rearrange("(ct p) -> p ct", p=128), w=["cw"])
                hb = [sb("hb%d" % i, [128, CH + 2]) for i in range(2)]; cb = [sb("cb%d" % i, [128, CH + 2]) for i in range(2)]; bb_ = [sb("bbb%d" % i, [128, CH]) for i in range(2)]
                acc = [sb("acc%d" % i, [128, CH]) for i in range(2)]; yo = [sb("yo%d" % i, [128, CH], BF16) for i in range(2)]
                k = 0
                for ct in range(c.CT):
                    for (s0, sl) in seqs(last):
                        for c0 in range(0, sl, CH):
                            n = min(CH, sl - c0); j = k % 2; k += 1
                            lo = 1 if c0 == 0 else 0; hi = n + 1 if c0 + n == sl else n + 2
                            a0 = s0 + c0 - 1 + lo; a1 = s0 + c0 - 1 + hi
                            op("dve", lambda e, j=j: e.memset(hb[j][:], 0.0), w=[("hb", j)])
                            dma("sync", hb[j][:, lo:hi], CVS[ct][:, a0:a1], w=[("hb", j)])
                            dma("sync", cb[j][:, lo:hi], CVS[2 * c.CT + ct][:, a0:a1], w=[("cb", j)])
                            dma("sync", bb_[j][:, 0:n], CVS[c.CT + ct][:, s0 + c0:s0 + c0 + n], w=[("bbb", j)])
                            op("dve", lambda e, j=j, lo=lo, hi=hi: e.tensor_tensor(out=hb[j][:, lo:hi], in0=hb[j][:, lo:hi], in1=cb[j][:, lo:hi], op=ALU.mult), r=[("cb", j), ("hb", j)], w=[("hb", j)])
                            op("dve", lambda e, j=j, n=n, ct=ct: e.tensor_scalar(out=acc[j][:, 0:n], in0=hb[j][:, 0:n], scalar1=cw[:, ct, 0:1], scalar2=None, op0=ALU.mult), r=[("hb", j), "cw"], w=[("acc", j)])
                            op("dve", lambda e, j=j, n=n, ct=ct: e.scalar_tensor_tensor(out=acc[j][:, 0:n], in0=hb[j][:, 1:n + 1], scalar=cw[:, ct, 1:2], in1=acc[j][:, 0:n], op0=ALU.mult, op1=ALU.add), r=[("hb", j), "cw", ("acc", j)], w=[("acc", j)])
                            op("dve", lambda e, j=j, n=n, ct=ct: e.scalar_tensor_tensor(out=acc[j][:, 0:n], in0=hb[j][:, 2:n + 2], scalar=cw[:, ct, 2:3], in1=acc[j][:, 0:n], op0=ALU.mult, op1=ALU.add), r=[("hb", j), "cw", ("acc", j)], w=[("acc", j)])
                            op("dve", lambda e, j=j, n=n: e.tensor_tensor(out=yo[j][:, 0:n], in0=acc[j][:, 0:n], in1=bb_[j][:, 0:n], op=ALU.mult), r=[("acc", j), ("bbb", j)], w=[("yo", j)])
                            dma("pool", YT[ct][:, s0 + c0:s0 + c0 + n], yo[j][:, 0:n], r=[("yo", j)], w=[("YTc", ct, s0, c0)])
            pg.barrier()

        def phase_attn(l, last):
            lam_init = 0.8 - 0.6 * math.exp(-0.3 * l)
            NKT = c.LS // 128; NKL = c.SEQ // 128
            with contextlib.ExitStack() as st:
                set_stack(st)
                lv = [sb("lv%d" % i, [128, 64]) for i in range(4)]; lt = sb("lt", [128, 8]); gsc = sb("gsc", [128, 1]); nlam = sb("nlam", [128, 1])
                for i, n_ in enumerate(("lam_q1", "lam_k1", "lam_q2", "lam_k2")):
                    dma("sync", lv[i][:], I[n_][l:l + 1, :].to_broadcast([128, 64]), w=[("lv", i)])
                for j in range(2):
                    op("dve", lambda e, j=j: e.tensor_tensor(out=lv[2 * j][:], in0=lv[2 * j][:], in1=lv[2 * j + 1][:], op=ALU.mult), r=[("lv", 2 * j), ("lv", 2 * j + 1)], w=[("lv", 2 * j)])
                    op("dve", lambda e, j=j: e.tensor_reduce(out=lt[:, j:j + 1], in_=lv[2 * j][:], axis=AX.X, op=ALU.add), r=[("lv", 2 * j)], w=["lt"])
                op("act", lambda e: e.activation(out=lt[:, 2:4], in_=lt[:, 0:2], func=AF.Exp), r=["lt"], w=["lt"])
                op("dve", lambda e: e.tensor_tensor(out=lt[:, 4:5], in0=lt[:, 3:4], in1=lt[:, 2:3], op=ALU.subtract), r=["lt"], w=["lt"])
                op("dve", lambda e: e.tensor_scalar(out=nlam[:], in0=lt[:, 4:5], scalar1=-lam_init, scalar2=None, op0=ALU.add), r=["lt"], w=["nlam"])
                dma("sync", gsc[:], I["subln_g"][l].rearrange("(p o) -> p o", o=1), w=["gsc"])
                op("dve", lambda e: e.tensor_scalar(out=gsc[:], in0=gsc[:], scalar1=1.0 - lam_init, scalar2=None, op0=ALU.mult), r=["gsc"], w=["gsc"])
                Kt = sb("Kt", [128, c.LS], BF16); Vh = sb("Vh", [128, NKT, 128], BF16)
                qt = [sb("qt%d" % i, [128, 512], BF16) for i in range(2)]
                pt_ = [sb("ptb%d" % i, [128, 2, 512], BF16) for i in range(3)]
                r0 = sb("r0", [128, 512]); r1 = sb("r1", [128, 512]); oa = sb("oa", [128, 512]); ob_ = sb("obb", [128, 512]); sq = sb("sq", [128, 512]); yb = [sb("yb%d" % i, [128, 512], BF16) for i in range(2)]
                pS = [ps("pS%d" % i, [128, 2, 512]) for i in range(2)]; pO = [ps("pOa%d" % i, [128, 512]) for i in range(2)]; pM = [ps("pM%d" % i, [128, 512]) for i in range(2)]
                uq = 0; us = 0; up = 0
                for b in range(c.B):
                    for h in range(c.H):
                        dma("sync", Kt[:, 0:c.SEQ], KTd[h][:, b * c.SEQ:(b + 1) * c.SEQ], w=["Kt"])
                        dma("sync", Kt[:, c.SEQ:], KTd[h][:, c.NLAT + b * c.CTX:c.NLAT + (b + 1) * c.CTX], w=["Kt"])
                        VS = min(16, NKL)
                        for v0 in range(0, NKL, VS):
                            dma("sync", Vh[:, v0:v0 + VS, :], Vd[b * c.SEQ + v0 * 128:b * c.SEQ + (v0 + VS) * 128, h * 128:(h + 1) * 128].rearrange("(kt p) e -> p kt e", p=128), w=[("Vh", v0)])
                        dma("sync", Vh[:, NKL:, :], Vd[c.NLAT + b * c.CTX:c.NLAT + (b + 1) * c.CTX, h * 128:(h + 1) * 128].rearrange("(kt p) e -> p kt e", p=128), w=[("Vh", "c")])
                        vkeys = [("Vh", v0) for v0 in range(0, NKL, VS)] + [("Vh", "c")]
                        units = [(b * c.SEQ + qg * 512, 512, 0, NKT) for qg in range(c.SEQ // 512)]
                        if not last: units.append((c.NLAT + b * c.CTX, c.CTX, NKL, NKT))
                        for (q0, nq, k0, k1) in units:
                            q_ = qt[uq % 2]; qk = ("qt", uq % 2); uq += 1
                            dma("sync", q_[:, 0:nq], QT[h][:, q0:q0 + nq], w=[qk])
                            for j2 in range(2):
                                js = slice(j2 * 64, (j2 + 1) * 64)
                                for kp in range(k0, k1, 2):
                                    S = pS[us % 2]; sk = ("pS", us % 2); us += 1
                                    P_ = pt_[up % 3]; pk = ("ptb", up % 3); up += 1
                                    for i in range(2):
                                        kt = kp + i
                                        op("pe", lambda e, S=S, i=i, kt=kt, js=js, q_=q_, nq=nq: e.matmul(S[:, i, 0:nq], lhsT=Kt[js, kt * 128:(kt + 1) * 128], rhs=q_[js, 0:nq], start=True, stop=True),
                                           r=["Kt", qk], w=[sk])
                                    op("act", lambda e, S=S, P_=P_, nq=nq: e.activation(out=P_[:, :, 0:nq], in_=S[:, :, 0:nq], func=AF.Exp, scale=0.125), r=[sk], w=[pk])
                                    for i in range(2):
                                        kt = kp + i
                                        op("pe", lambda e, P_=P_, i=i, kt=kt, j2=j2, nq=nq, k0=k0, k1=k1: e.matmul(pO[j2][:, 0:nq], lhsT=Vh[:, kt, :], rhs=P_[:, i, 0:nq], start=(kt == k0), stop=(kt == k1 - 1)),
                                           r=[pk] + vkeys, w=[("pOa", j2)])
                                        op("pe", lambda e, P_=P_, i=i, kt=kt, j2=j2, nq=nq, k0=k0, k1=k1: e.matmul(pM[j2][:, 0:nq], lhsT=ones_b[:], rhs=P_[:, i, 0:nq], start=(kt == k0), stop=(kt == k1 - 1)),
                                           r=[pk, "ones_b"], w=[("pM", j2)])
                            nn = nq
                            op("dve", lambda e, nn=nn: e.reciprocal(out=r0[:, 0:nn], in_=pM[0][:, 0:nn]), r=[("pM", 0)], w=["r0"])
                            op("dve", lambda e, nn=nn: e.reciprocal(out=r1[:, 0:nn], in_=pM[1][:, 0:nn]), r=[("pM", 1)], w=["r1"])
                            op("dve", lambda e, nn=nn: e.tensor_tensor(out=oa[:, 0:nn], in0=pO[0][:, 0:nn], in1=r0[:, 0:nn], op=ALU.mult), r=[("pOa", 0), "r0"], w=["oa"])
                            op("dve", lambda e, nn=nn: e.tensor_tensor(out=ob_[:, 0:nn], in0=pO[1][:, 0:nn], in1=r1[:, 0:nn], op=ALU.mult), r=[("pOa", 1), "r1"], w=["obb"])
                            op("dve", lambda e, nn=nn: e.scalar_tensor_tensor(out=oa[:, 0:nn], in0=ob_[:, 0:nn], scalar=nlam[:, 0:1], in1=oa[:, 0:nn], op0=ALU.mult, op1=ALU.add), r=["obb", "oa", "nlam"], w=["oa"])
                            op("dve", lambda e, nn=nn: e.tensor_tensor(out=sq[:, 0:nn], in0=oa[:, 0:nn], in1=oa[:, 0:nn], op=ALU.mult), r=["oa"], w=["sq"])
                            S = pS[us % 2]; sk = ("pS", us % 2); us += 1
                            op("pe", lambda e, S=S, nn=nn: e.matmul(S[:, 0, 0:nn], lhsT=ones_f[:], rhs=sq[:, 0:nn], start=True, stop=True), r=["sq", "ones_f"], w=[sk])
                            op("act", lambda e, S=S, nn=nn: e.activation(out=r0[:, 0:nn], in_=S[:, 0, 0:nn], func=AF.Sqrt, scale=1.0 / 128, bias=1e-6), r=[sk], w=["r0"])
                            op("dve", lambda e, nn=nn: e.reciprocal(out=r1[:, 0:nn], in_=r0[:, 0:nn]), r=["r0"], w=["r1"])
                            y_ = yb[uq % 2]; yk = ("yb", uq % 2)
                            op("dve", lambda e, nn=nn, y_=y_: e.scalar_tensor_tensor(out=y_[:, 0:nn], in0=oa[:, 0:nn], scalar=gsc[:, 0:1], in1=r1[:, 0:nn], op0=ALU.mult, op1=ALU.mult), r=["oa", "gsc", "r1"], w=[yk])
                            dma("pool", YT[c.CT + h][:, q0:q0 + nq], y_[:, 0:nn], r=[yk], w=[("YTa", h, q0)])
            pg.barrier()

        def phase_ssm(l, last):
            G_ = c.G; NPT = G_; LS = c.LS; SEQ = c.SEQ; CTX = c.CTX; UW = CTX + SEQ + CTX
            NSEG = SSM_NSEG; SEG = LS // NSEG; CH2 = min(2048, SEG)
            with contextlib.ExitStack() as st:
                set_stack(st)
                PR = {n_: sb("pr_" + n_, [128, NPT]) for n_ in ("are", "aim", "ldt", "dt", "r", "th", "k", "t", "m", "cs", "sn", "abr", "abi", "den", "nr", "cr", "ci", "x1", "x2")}
                ki = sb("ki", [128, NPT], I32)
                pw = sb("pw", [128, NPT, 15, 2])
                dma("sync", PR["are"][:], I["ssm_a_re"][l].rearrange("(t p) -> p t", p=128), w=["are"])
                dma("sync", PR["aim"][:], I["ssm_a_im"][l].rearrange("(t p) -> p t", p=128), w=["aim"])
                for m in range(2):
                    dma("sync", PR["ldt"][m * 64:(m + 1) * 64, :], I["ssm_log_dt"][l:l + 1, :].rearrange("o (t m) -> o t m", m=2)[:, :, m].to_broadcast([64, NPT]), w=[("ldt", m)])
                def V(n_): return PR[n_][:]
                def tt_(o, a, b_, o_): op("dve", lambda e: e.tensor_tensor(out=V(o), in0=V(a), in1=V(b_), op=o_), r=["prm"], w=["prm"])
                def ts_(o, a, s1, o1, s2=None, o2=None):
                    if o2 is None: op("dve", lambda e: e.tensor_scalar(out=V(o), in0=V(a), scalar1=s1, scalar2=None, op0=o1), r=["prm"], w=["prm"])
                    else: op("dve", lambda e: e.tensor_scalar(out=V(o), in0=V(a), scalar1=s1, scalar2=s2, op0=o1, op1=o2), r=["prm"], w=["prm"])
                op("dve", lambda e: e.tensor_scalar(out=V("x1"), in0=V("ldt"), scalar1=0.125, scalar2=None, op0=ALU.mult), r=[("ldt", 0), ("ldt", 1)], w=["prm"])
                ts_("dt", "x1", 1.0 / 14, ALU.mult, 1.0, ALU.add)
                for kk in range(13, 0, -1):
                    tt_("dt", "dt", "x1", ALU.mult); ts_("dt", "dt", 1.0 / kk, ALU.mult, 1.0, ALU.add)
                for _ in range(3): tt_("dt", "dt", "dt", ALU.mult)
                op("dve", lambda e: e.tensor_tensor(out=V("x1"), in0=V("dt"), in1=V("are"), op=ALU.mult), r=["prm", "are"], w=["prm"])
                op("act", lambda e: e.activation(out=V("r"), in_=V("x1"), func=AF.Exp), r=["prm"], w=["prm"])
                op("dve", lambda e: e.tensor_tensor(out=V("th"), in0=V("dt"), in1=V("aim"), op=ALU.mult), r=["prm", "aim"], w=["prm"])
                C1 = 6.28125; C2 = TWO_PI - C1
                ts_("k", "th", 1.0 / TWO_PI, ALU.mult)
                op("dve", lambda e: e.tensor_copy(out=ki[:], in_=V("k")), r=["prm"], w=["ki"])
                op("dve", lambda e: e.tensor_copy(out=V("k"), in_=ki[:]), r=["ki"], w=["prm"])
                op("dve", lambda e: e.scalar_tensor_tensor(out=V("t"), in0=V("k"), scalar=-C1, in1=V("th"), op0=ALU.mult, op1=ALU.add), r=["prm"], w=["prm"])
                op("dve", lambda e: e.scalar_tensor_tensor(out=V("t"), in0=V("k"), scalar=-C2, in1=V("t"), op0=ALU.mult, op1=ALU.add), r=["prm"], w=["prm"])
                ts_("m", "t", math.pi, ALU.is_gt)
                op("dve", lambda e: e.scalar_tensor_tensor(out=V("t"), in0=V("m"), scalar=-C1, in1=V("t"), op0=ALU.mult, op1=ALU.add), r=["prm"], w=["prm"])
                op("dve", lambda e: e.scalar_tensor_tensor(out=V("t"), in0=V("m"), scalar=-C2, in1=V("t"), op0=ALU.mult, op1=ALU.add), r=["prm"], w=["prm"])
                ts_("m", "t", -math.pi, ALU.is_lt)
                op("dve", lambda e: e.scalar_tensor_tensor(out=V("t"), in0=V("m"), scalar=C1, in1=V("t"), op0=ALU.mult, op1=ALU.add), r=["prm"], w=["prm"])
                op("dve", lambda e: e.scalar_tensor_tensor(out=V("t"), in0=V("m"), scalar=C2, in1=V("t"), op0=ALU.mult, op1=ALU.add), r=["prm"], w=["prm"])
                ts_("t", "t", 0.125, ALU.mult)
                tt_("x2", "t", "t", ALU.mult)
                ts_("sn", "x2", -1.0 / 110, ALU.mult, 1.0, ALU.add)
                for dnm in (72, 42, 20, 6):
                    tt_("sn", "sn", "x2", ALU.mult); ts_("sn", "sn", -1.0 / dnm, ALU.mult, 1.0, ALU.add)
                tt_("sn", "sn", "t", ALU.mult)
                ts_("cs", "x2", -1.0 / 132, ALU.mult, 1.0, ALU.add)
                for dnm in (90, 56, 30, 12, 2):
                    tt_("cs", "cs", "x2", ALU.mult); ts_("cs", "cs", -1.0 / dnm, ALU.mult, 1.0, ALU.add)
                for _ in range(3):
                    tt_("x1", "sn", "cs", ALU.mult); tt_("m", "sn", "sn", ALU.mult)
                    ts_("sn", "x1", 2.0, ALU.mult); ts_("cs", "m", -2.0, ALU.mult, 1.0, ALU.add)
                def renorm(ca, sa):
                    op("dve", lambda e: e.tensor_tensor(out=V("x1"), in0=ca, in1=ca, op=ALU.mult), r=["prm", "pw"], w=["prm"])
                    op("dve", lambda e: e.tensor_tensor(out=V("x2"), in0=sa, in1=sa, op=ALU.mult), r=["prm", "pw"], w=["prm"])
                    tt_("x1", "x1", "x2", ALU.add); ts_("x1", "x1", -0.5, ALU.mult, 1.5, ALU.add)
                    op("dve", lambda e: e.tensor_tensor(out=ca, in0=ca, in1=V("x1"), op=ALU.mult), r=["prm", "pw"], w=["prm", "pw"])
                    op("dve", lambda e: e.tensor_tensor(out=sa, in0=sa, in1=V("x1"), op=ALU.mult), r=["prm", "pw"], w=["prm", "pw"])
                renorm(V("cs"), V("sn"))
                tt_("abr", "r", "cs", ALU.mult); tt_("abi", "r", "sn", ALU.mult)
                tt_("den", "are", "are", ALU.mult); tt_("x1", "aim", "aim", ALU.mult); tt_("den", "den", "x1", ALU.add)
                op("dve", lambda e: e.reciprocal(out=V("den"), in_=V("den")), r=["prm"], w=["prm"])
                ts_("nr", "abr", -1.0, ALU.add)
                tt_("x1", "nr", "are", ALU.mult); tt_("x2", "abi", "aim", ALU.mult); tt_("cr", "x1", "x2", ALU.add); tt_("cr", "cr", "den", ALU.mult)
                tt_("x1", "abi", "are", ALU.mult); tt_("x2", "nr", "aim", ALU.mult); tt_("ci", "x1", "x2", ALU.subtract); tt_("ci", "ci", "den", ALU.mult)
                op("dve", lambda e: e.tensor_copy(out=pw[:, :, 0, 0], in_=V("cs")), r=["prm"], w=["pw"])
                op("dve", lambda e: e.tensor_copy(out=pw[:, :, 0, 1], in_=V("sn")), r=["prm"], w=["pw"])
                for k in range(14):
                    a_, b_ = pw[:, :, k, 0], pw[:, :, k, 1]
                    op("dve", lambda e, a_=a_: e.tensor_tensor(out=V("x1"), in0=a_, in1=a_, op=ALU.mult), r=["pw", "prm"], w=["prm"])
                    op("dve", lambda e, b_=b_: e.tensor_tensor(out=V("x2"), in0=b_, in1=b_, op=ALU.mult), r=["pw", "prm"], w=["prm"])
                    op("dve", lambda e, k=k: e.tensor_tensor(out=pw[:, :, k + 1, 0], in0=V("x1"), in1=V("x2"), op=ALU.subtract), r=["prm"], w=["pw"])
                    op("dve", lambda e, a_=a_, b_=b_: e.tensor_tensor(out=V("x1"), in0=a_, in1=b_, op=ALU.mult), r=["pw", "prm"], w=["prm"])
                    op("dve", lambda e, k=k: e.tensor_scalar(out=pw[:, :, k + 1, 1], in0=V("x1"), scalar1=2.0, scalar2=None, op0=ALU.mult), r=["prm"], w=["pw"])
                    renorm(pw[:, :, k + 1, 0], pw[:, :, k + 1, 1])
                dsk = sb("dsk", [128, c.GT]); dma("sync", dsk[:], I["ssm_d"][l].rearrange("(t p) -> p t", p=128), w=["dsk"])
                if DEBUG and l == 0:
                    for i_, n_ in enumerate(("cs", "sn", "r", "cr", "ci", "dt")): dma("pool", DBP[i_], V(n_), r=["prm"], w=[("dbp", i_)])
                Uc = sb("Uc", [128, UW], BF16); Yc = sb("Yc", [128, UW]); Ec = sb("Ec", [128, LS]); Es = sb("Es", [128, LS])
                Vr = sb("Vr", [128, SEG]); Vi = sb("Vi", [128, SEG]); Gr = sb("Gr", [128, SEG]); Gi = sb("Gi", [128, SEG])
                Hr = sb("Hr", [128, SEG], BF16); Hi = sb("Hi", [128, SEG], BF16)
                ta = sb("ta", [128, 512]); tb = sb("tb", [128, 512]); car = sb("car", [128, 2])
                bre = sb("bre", [128, 16]); bim = sb("bim", [128, 16]); bx = sb("bx", [128, 16]); by = sb("by", [128, 16])
                Bp = [sb("Bp%d" % i, [128, 128]) for i in range(2)]; LB = [sb("LB%d" % i, [128, 128], BF16) for i in range(2)]
                Cf = [sb("Cf%d" % i, [128, 128]) for i in range(2)]; LC = [sb("LC%d" % i, [128, 128], BF16) for i in range(2)]
                pB = [ps("pB%d" % i, [128, 512]) for i in range(2)]; pY = ps("pY", [128, 512]); pTt = ps("pTt", [128, 128])
                for gt in range(c.GT):
                    for b in range(c.B):
                        lat0 = b * SEQ; cx0 = c.NLAT + b * CTX
                        dma("sync", Uc[:, 0:CTX], UT[gt][:, cx0:cx0 + CTX], w=["Uc"]); dma("sync", Uc[:, CTX + SEQ:], UT[gt][:, cx0:cx0 + CTX], w=["Uc"])
                        for c0 in range(0, SEQ, 1024):
                            dma("sync", Uc[:, CTX + c0:CTX + c0 + 1024], UT[gt][:, lat0 + c0:lat0 + c0 + 1024], w=["Uc"])
                        for c0 in range(0, CTX + SEQ, 2048):
                            n = min(2048, CTX + SEQ - c0)
                            op("dve", lambda e, c0=c0, n=n, gt=gt: e.tensor_scalar(out=Yc[:, c0:c0 + n], in0=Uc[:, c0:c0 + n], scalar1=dsk[:, gt:gt + 1], scalar2=None, op0=ALU.mult), r=["Uc", "dsk"], w=["Yc"])
                        op("dve", lambda e: e.memset(Yc[:, CTX + SEQ:], 0.0), w=["Yc"])
                        for d in SSM_DIRS:
                            for jj in range(4):
                                pti = (d * G_ + 8 * gt) // 2 + jj
                                gl = [(2 * jj) , (2 * jj + 1)]
                                dma("sync", bre[:], I["ssm_b_re"][l][pti * 128:(pti + 1) * 128, :], w=["bre"]); dma("sync", bim[:], I["ssm_b_im"][l][pti * 128:(pti + 1) * 128, :], w=["bim"])
                                crp, cip = PR["cr"][:, pti:pti + 1], PR["ci"][:, pti:pti + 1]
                                for ri in range(2):
                                    op("dve", lambda e, ri=ri: e.memset(Bp[ri][:], 0.0), w=[("Bp", ri)])
                                    op("dve", lambda e, ri=ri: e.memset(Cf[ri][:], 0.0), w=[("Cf", ri)])
                                op("dve", lambda e, cip=cip: e.tensor_scalar(out=bx[:], in0=bim[:], scalar1=cip, scalar2=-1.0, op0=ALU.mult, op1=ALU.mult), r=["bim", "prm"], w=["bx"])
                                op("dve", lambda e, cip=cip: e.tensor_scalar(out=by[:], in0=bre[:], scalar1=cip, scalar2=None, op0=ALU.mult), r=["bre", "prm"], w=["by"])
                                for m in range(2):
                                    rs_ = slice(m * 64, (m + 1) * 64); cs_ = slice(gl[m] * 16, gl[m] * 16 + 16)
                                    op("dve", lambda e, rs_=rs_, cs_=cs_, crp=crp: e.scalar_tensor_tensor(out=Bp[0][rs_, cs_], in0=bre[rs_, :], scalar=crp[rs_, :], in1=bx[rs_, :], op0=ALU.mult, op1=ALU.add), r=["bre", "bx", "prm"], w=[("Bp", 0)])
                                    op("dve", lambda e, rs_=rs_, cs_=cs_, crp=crp: e.scalar_tensor_tensor(out=Bp[1][rs_, cs_], in0=bim[rs_, :], scalar=crp[rs_, :], in1=by[rs_, :], op0=ALU.mult, op1=ALU.add), r=["bim", "by", "prm"], w=[("Bp", 1)])
                                    qi = 2 * pti + m
                                    dma("sync", Cf[0][rs_, cs_], I["ssm_c_re"][l][qi].rearrange("p n -> n p"), w=[("Cf", 0)])
                                    dma("sync", Cf[1][rs_, cs_], I["ssm_c_im"][l][qi].rearrange("p n -> n p"), w=[("Cf", 1)])
                                for ri in range(2):
                                    op("pe", lambda e, ri=ri: e.transpose(out=pTt[:], in_=Bp[ri][:], identity=ident[:]), r=[("Bp", ri), "ident"], w=["pTt"])
                                    op("act", lambda e, ri=ri: e.activation(out=LB[ri][:], in_=pTt[:], func=AF.Identity), r=["pTt"], w=[("LB", ri)])
                                op("act", lambda e: e.activation(out=LC[0][:], in_=Cf[0][:], func=AF.Identity), r=[("Cf", 0)], w=[("LC", 0)])
                                op("act", lambda e: e.activation(out=LC[1][:], in_=Cf[1][:], func=AF.Identity, scale=-1.0), r=[("Cf", 1)], w=[("LC", 1)])
                                i1 = 0 if d == 0 else LS - 1
                                op("dve", lambda e, i1=i1: e.memset(Ec[:, i1:i1 + 1], 1.0), w=["E"]); op("dve", lambda e, i1=i1: e.memset(Es[:, i1:i1 + 1], 0.0), w=["E"])
                                for k in range(15):
                                    n0 = 1 << k
                                    if n0 >= LS: break
                                    m_ = min(n0, LS - n0)
                                    ck, sk_ = pw[:, pti, k, 0:1], pw[:, pti, k, 1:2]
                                    for c0 in range(0, m_, CH2):
                                        n = min(CH2, m_ - c0)
                                        if d == 0: a = slice(c0, c0 + n); o_ = slice(n0 + c0, n0 + c0 + n)
                                        else: a = slice(LS - m_ + c0, LS - m_ + c0 + n); o_ = slice(LS - n0 - m_ + c0, LS - n0 - m_ + c0 + n)
                                        op("dve", lambda e, a=a, n=n, sk_=sk_: e.tensor_scalar(out=Vr[:, 0:n], in0=Es[:, a], scalar1=sk_, scalar2=None, op0=ALU.mult), r=["E", "pw"], w=["Vr"])
                                        op("dve", lambda e, a=a, n=n, ck=ck: e.tensor_scalar(out=Vi[:, 0:n], in0=Es[:, a], scalar1=ck, scalar2=None, op0=ALU.mult), r=["E", "pw"], w=["Vi"])
                                        op("dve", lambda e, a=a, o_=o_, n=n, ck=ck: e.scalar_tensor_tensor(out=Ec[:, o_], in0=Ec[:, a], scalar=ck, in1=Vr[:, 0:n], op0=ALU.mult, op1=ALU.subtract), r=["E", "pw", "Vr"], w=["E"])
                                        op("dve", lambda e, a=a, o_=o_, n=n, sk_=sk_: e.scalar_tensor_tensor(out=Es[:, o_], in0=Ec[:, a], scalar=sk_, in1=Vi[:, 0:n], op0=ALU.mult, op1=ALU.add), r=["E", "pw", "Vi"], w=["E"])
                                base = 0 if d == 0 else CTX
                                rb = PR["r"][:, pti:pti + 1]
                                for si in range(NSEG):
                                    s_ = si if d == 0 else NSEG - 1 - si
                                    i0 = s_ * SEG
                                    def Ev(T, a0, n):
                                        return T[:, a0:a0 + n]
                                    for c0 in range(0, SEG, 512):
                                        n = min(512, SEG - c0); u0 = base + i0 + c0; sl_ = slice(c0, c0 + n)
                                        for ri in range(2):
                                            op("pe", lambda e, ri=ri, u0=u0, n=n: e.matmul(pB[ri][:, 0:n], lhsT=LB[ri][:], rhs=Uc[:, u0:u0 + n], start=True, stop=True), r=[("LB", ri), "Uc"], w=[("pB", ri)])
                                        ec = Ev(Ec, i0 + c0, n); es = Ev(Es, i0 + c0, n)
                                        op("dve", lambda e, n=n, ec=ec: e.tensor_tensor(out=ta[:, 0:n], in0=pB[0][:, 0:n], in1=ec, op=ALU.mult), r=[("pB", 0), "E"], w=["ta"])
                                        op("dve", lambda e, n=n, es=es: e.tensor_tensor(out=tb[:, 0:n], in0=pB[1][:, 0:n], in1=es, op=ALU.mult), r=[("pB", 1), "E"], w=["tb"])
                                        op("dve", lambda e, n=n, sl_=sl_: e.tensor_tensor(out=Vr[:, sl_], in0=ta[:, 0:n], in1=tb[:, 0:n], op=ALU.add), r=["ta", "tb"], w=["Vr"])
                                        op("dve", lambda e, n=n, ec=ec: e.tensor_tensor(out=ta[:, 0:n], in0=pB[1][:, 0:n], in1=ec, op=ALU.mult), r=[("pB", 1), "E"], w=["ta"])
                                        op("dve", lambda e, n=n, es=es: e.tensor_tensor(out=tb[:, 0:n], in0=pB[0][:, 0:n], in1=es, op=ALU.mult), r=[("pB", 0), "E"], w=["tb"])
                                        op("dve", lambda e, n=n, sl_=sl_: e.tensor_tensor(out=Vi[:, sl_], in0=ta[:, 0:n], in1=tb[:, 0:n], op=ALU.subtract), r=["ta", "tb"], w=["Vi"])
                                    def dv(T, d_): return T[:, :] if d_ == 0 else T[:, ::-1]
                                    for (Gt, Vt, gk, vk, ci_) in ((Gr, Vr, "Gr", "Vr", 0), (Gi, Vi, "Gi", "Vi", 1)):
                                        init = 0.0 if si == 0 else car[:, ci_:ci_ + 1]
                                        go_, vo_, r_ = dv(Gt, d), dv(Vt, d), rb.to_broadcast([128, SEG])
                                        op("dve", lambda e, go_=go_, vo_=vo_, r_=r_, init=init: e.tensor_tensor_scan(out=go_, data0=r_, data1=vo_, initial=init, op0=ALU.mult, op1=ALU.add),
                                           r=[vk, "prm", "car"], w=[gk])
                                    edge = (SEG - 1) if d == 0 else 0
                                    op("dve", lambda e, edge=edge: e.tensor_copy(out=car[:, 0:1], in_=Gr[:, edge:edge + 1]), r=["Gr"], w=["car"])
                                    op("dve", lambda e, edge=edge: e.tensor_copy(out=car[:, 1:2], in_=Gi[:, edge:edge + 1]), r=["Gi"], w=["car"])
                                    for c0 in range(0, SEG, 512):
                                        n = min(512, SEG - c0); sl_ = slice(c0, c0 + n); u0 = base + i0 + c0
                                        ec = Ev(Ec, i0 + c0, n); es = Ev(Es, i0 + c0, n)
                                        op("dve", lambda e, n=n, sl_=sl_, ec=ec: e.tensor_tensor(out=ta[:, 0:n], in0=Gr[:, sl_], in1=ec, op=ALU.mult), r=["Gr", "E"], w=["ta"])
                                        op("dve", lambda e, n=n, sl_=sl_, es=es: e.tensor_tensor(out=tb[:, 0:n], in0=Gi[:, sl_], in1=es, op=ALU.mult), r=["Gi", "E"], w=["tb"])
                                        op("dve", lambda e, n=n, sl_=sl_: e.tensor_tensor(out=Hr[:, sl_], in0=ta[:, 0:n], in1=tb[:, 0:n], op=ALU.subtract), r=["ta", "tb"], w=["Hr"])
                                        op("dve", lambda e, n=n, sl_=sl_, es=es: e.tensor_tensor(out=ta[:, 0:n], in0=Gr[:, sl_], in1=es, op=ALU.mult), r=["Gr", "E"], w=["ta"])
                                        op("dve", lambda e, n=n, sl_=sl_, ec=ec: e.tensor_tensor(out=tb[:, 0:n], in0=Gi[:, sl_], in1=ec, op=ALU.mult), r=["Gi", "E"], w=["tb"])
                                        op("dve", lambda e, n=n, sl_=sl_: e.tensor_tensor(out=Hi[:, sl_], in0=ta[:, 0:n], in1=tb[:, 0:n], op=ALU.add), r=["ta", "tb"], w=["Hi"])
                                        op("pe", lambda e, n=n, sl_=sl_: e.matmul(pY[:, 0:n], lhsT=LC[0][:], rhs=Hr[:, sl_], start=True, stop=False), r=[("LC", 0), "Hr"], w=["pY"])
                                        op("pe", lambda e, n=n, sl_=sl_: e.matmul(pY[:, 0:n], lhsT=LC[1][:], rhs=Hi[:, sl_], start=False, stop=True), r=[("LC", 1), "Hi"], w=["pY"])
                                        op("dve", lambda e, n=n, u0=u0: e.tensor_tensor(out=Yc[:, u0:u0 + n], in0=Yc[:, u0:u0 + n], in1=pY[:, 0:n], op=ALU.add), r=["pY", "Yc"], w=["Yc"])
                        if not last:
                            op("dve", lambda e: e.tensor_tensor(out=Yc[:, 0:CTX], in0=Yc[:, 0:CTX], in1=Yc[:, CTX + SEQ:], op=ALU.add), r=["Yc"], w=["Yc"])
                        pieces = [(CTX + c0, lat0 + c0, min(CH2, SEQ - c0)) for c0 in range(0, SEQ, CH2)]
                        if not last: pieces.append((0, cx0, CTX))
                        for (y0, d0, n) in pieces:
                            ys = Yc[:, y0:y0 + n]
                            op("dve", lambda e, ys=ys, n=n: e.tensor_tensor(out=Vr[:, 0:n], in0=ys, in1=ys, op=ALU.mult), r=["Yc"], w=["Vr"])
                            op("dve", lambda e, ys=ys, n=n: e.tensor_tensor(out=Vr[:, 0:n], in0=Vr[:, 0:n], in1=ys, op=ALU.mult), r=["Yc", "Vr"], w=["Vr"])
                            op("dve", lambda e, ys=ys, n=n: e.scalar_tensor_tensor(out=Vr[:, 0:n], in0=Vr[:, 0:n], scalar=0.044715, in1=ys, op0=ALU.mult, op1=ALU.add), r=["Yc", "Vr"], w=["Vr"])
                            op("act", lambda e, n=n: e.activation(out=Vi[:, 0:n], in_=Vr[:, 0:n], func=AF.Sigmoid, scale=1.5957691216057308), r=["Vr"], w=["Vi"])
                            op("dve", lambda e, ys=ys, n=n: e.tensor_tensor(out=Hr[:, 0:n], in0=ys, in1=Vi[:, 0:n], op=ALU.mult), r=["Yc", "Vi"], w=["Hr"])
                            dma("pool", YG[gt][:, d0:d0 + n], Hr[:, 0:n], r=["Hr"], w=[("YG", gt, d0)])
            pg.barrier()

        def phase_glu_out(l, last):
            ng = c.NGL if last else c.NG
            GT = c.GT
            with contextlib.ExitStack() as st:
                set_stack(st)
                yg = sb("yg", [128, GT, 512], BF16); wg = [sb("wg%d" % i, [128, GT, 128], BF16) for i in range(2)]
                sg = [sb("sg%d" % i, [128, 512]) for i in range(2)]; yo = [sb("yso%d" % i, [128, 512], BF16) for i in range(2)]
                yT = sb("yT", [128, KT, 512], BF16); wo = [sb("wo%d" % i, [128, KT, 512], BF16) for i in range(2)]
                gb = sb("gb2", [128, 512]); xc = [sb("xc2%d" % i, [128, 512]) for i in range(2)]; tm = [sb("tm2%d" % i, [128, 512]) for i in range(2)]
                pZ = [ps("pZ%d" % i, [128, 512]) for i in range(2)]; pO = [ps("pOo%d" % i, [128, 512]) for i in range(4)]
                for g in range(ng):
                    row = grow(g); cols = slice(g * 512, (g + 1) * 512)
                    dma("sync", yg[:], YG.rearrange("t p n -> p t n")[:, :, cols], w=["yg"])
                    for go in range(GT):
                        w_ = wg[go % 2]; wk = ("wg", go % 2); pz = pZ[go % 2]; zk = ("pZ", go % 2)
                        dma("sync", w_[:], Wg[l][go], w=[wk])
                        for kt in range(GT):
                            op("pe", lambda e, pz=pz, w_=w_, kt=kt: e.matmul(pz[:], lhsT=w_[:, kt, :], rhs=yg[:, kt, :], start=(kt == 0), stop=(kt == GT - 1)), r=[wk, "yg"], w=[zk])
                        op("act", lambda e, go=go, pz=pz: e.activation(out=sg[go % 2][:], in_=pz[:], func=AF.Sigmoid), r=[zk], w=[("sg", go % 2)])
                        op("dve", lambda e, go=go: e.tensor_tensor(out=yo[go % 2][:], in0=yg[:, go, :], in1=sg[go % 2][:], op=ALU.mult), r=["yg", ("sg", go % 2)], w=[("yso", go % 2)])
                        dma("pool", YT[c.CT + c.H + go][:, cols], yo[go % 2][:], r=[("yso", go % 2)], w=[("YTs", go, g)])
                pg.barrier()
                k = 0
                for g in range(ng):
                    row = grow(g); cols = slice(g * 512, (g + 1) * 512)
                    dma("sync", yT[:], YT.rearrange("t p n -> p t n")[:, :, cols], w=["yT"])
                    for n_ in range(D // 512):
                        w_ = wo[k % 2]; wk = ("wo", k % 2); k += 1
                        dma("sync", w_[:], Wo[l][n_], w=[wk])
                        dma("sync", gb[:], MOD[l][row:row + 1, 5 * D + n_ * 512:5 * D + (n_ + 1) * 512].to_broadcast([128, 512]), w=["gb2"])
                        for tt in range(4):
                            for kt in range(KT):
                                op("pe", lambda e, w_=w_, kt=kt, tt=tt: e.matmul(pO[tt][:], lhsT=yT[:, kt, tt * 128:(tt + 1) * 128], rhs=w_[:, kt, :], start=(kt == 0), stop=(kt == KT - 1)),
                                   r=[wk, "yT"], w=[("pOo", tt)])
                        for tt in range(4):
                            t0 = g * 512 + tt * 128; xcc = xc[tt % 2]; tmm = tm[tt % 2]
                            dma("sync", xcc[:], XS(t0, t0 + 128)[:, n_ * 512:(n_ + 1) * 512], w=[("xc2", tt % 2)])
                            op("dve", lambda e, tmm=tmm, tt=tt: e.tensor_tensor(out=tmm[:], in0=pO[tt][:], in1=gb[:], op=ALU.mult), r=[("pOo", tt), "gb2"], w=[("tm2", tt % 2)])
                            op("dve", lambda e, tmm=tmm, xcc=xcc: e.tensor_tensor(out=tmm[:], in0=tmm[:], in1=xcc[:], op=ALU.add), r=[("tm2", tt % 2), ("xc2", tt % 2)], w=[("tm2", tt % 2)])
                            dma("pool", XS(t0, t0 + 128)[:, n_ * 512:(n_ + 1) * 512], tmm[:], r=[("tm2", tt % 2)], w=[("Xw2", g, n_, tt)])
            pg.barrier()

        def phase_mixer(l, last):
            phase_inproj(l, last)
            if MIXER_PARTS & 1: phase_conv(l, last)
            if MIXER_PARTS & 2: phase_attn(l, last)
            if MIXER_PARTS & 4: phase_ssm(l, last)
            phase_glu_out(l, last)

        phase_mod()
        for l in range(L):
            load_modp(l)
            last = (l == c.DEPTH - 1)
            phase_ffn(l, "a", c.NG)
            if DEBUG and l == 0:
                dma("sync", DBM, MOD[0], w=["dbm"])
                for b_ in range(c.B): dma("sync", DBX[b_ * c.SEQ:(b_ + 1) * c.SEQ, :], XP[b_], w=[("dbx", b_)])
                dma("sync", DBX[c.NLAT:, :], XP[c.B], w=[("dbx", 9)]); pg.barrier()
            phase_mixer(l, last)
            if DEBUG and l == 0:
                for t_ in range(KT): dma("pool", DBY[t_], YT[t_], w=[("dby", t_)])
                for t_ in range(c.GT): dma("pool", DBG_[t_], YG[t_], w=[("dbg", t_)])
                pg.barrier()
            phase_ffn(l, "b", c.NGL if last else c.NG)
        phase_final()
        cst.close()
        pg.emit()
    return nc


def MIXER(c, nc, pg, I, env):
    return lambda l, last: None


def host_consts(cfg):
    c = cfg
    ident = np.eye(128, dtype=np.float32)
    rotm = np.zeros((128, 128), np.float32)
    for j in range(2):
        o = j * 64
        for i in range(16):
            rotm[o + 16 + i, o + i] = -1.0
            rotm[o + i, o + 16 + i] = 1.0
            rotm[o + 48 + i, o + 32 + i] = -1.0
            rotm[o + 32 + i, o + 48 + i] = 1.0
    rows = c.SEQ // 64
    row = np.repeat(np.arange(rows), 64).astype(np.float32); col = np.tile(np.arange(64), rows).astype(np.float32)
    inv = (10000.0 ** (-np.arange(0, 32, 2, dtype=np.float32) / 32)).astype(np.float32)
    ar = row[:, None] * inv[None]; ac = col[:, None] * inv[None]
    ang = np.concatenate([ar, ar, ac, ac], -1)
    cos = np.cos(ang).astype(np.float32).T; sin = np.sin(ang).astype(np.float32).T
    return dict(ident=ident, rotm=rotm, ropec=np.ascontiguousarray(np.concatenate([cos, cos], 0)), ropes=np.ascontiguousarray(np.concatenate([sin, sin], 0)))


def run(cfg, inputs, nlayers=None):
    c = cfg
    nc = build(cfg, nlayers)
    m = {}
    f = lambda a: np.ascontiguousarray(np.asarray(a, dtype=np.float32))
    m["x"] = f(inputs["x"]).reshape(c.NLAT, c.D); m["c"] = f(inputs["c"]); m["ctx"] = f(inputs["ctx"]).reshape(c.NCTX, c.D)
    m["c_ctx"] = f(inputs["c_ctx"]).reshape(1, c.D); m["final_g"] = f(inputs["final_g"]).reshape(1, c.D)
    for k in ("w_ada", "b_ada", "ffa_w1", "ffa_w3", "ffa_w2", "ffb_w1", "ffb_w3", "ffb_w2", "w_in", "w_out", "conv_w", "lam_q1", "lam_k1",
              "lam_q2", "lam_k2", "subln_g", "ssm_d", "ssm_w_glu"):
        m[k] = f(inputs[k])
    G = c.G
    m["ssm_a_re"] = f(inputs["ssm_a_re"]).reshape(c.DEPTH, 2 * G * 64); m["ssm_a_im"] = f(inputs["ssm_a_im"]).reshape(c.DEPTH, 2 * G * 64)
    m["ssm_log_dt"] = f(inputs["ssm_log_dt"]).reshape(c.DEPTH, 2 * G)
    m["ssm_b_re"] = f(inputs["ssm_b_re"]).reshape(c.DEPTH, 2 * G * 64, 16); m["ssm_b_im"] = f(inputs["ssm_b_im"]).reshape(c.DEPTH, 2 * G * 64, 16)
    m["ssm_c_re"] = f(inputs["ssm_c_re"]).reshape(c.DEPTH, 2 * G, 16, 64); m["ssm_c_im"] = f(inputs["ssm_c_im"]).reshape(c.DEPTH, 2 * G, 16, 64)
    m.update(host_consts(c))
    res = run_bass_kernel_spmd(nc, [m], core_ids=[0])
    if DEBUG:
        global LAST; LAST = res.results[0]
    return res.results[0]["out"].reshape(c.B, c.SEQ, c.D)


def kernel(**inputs):
    return run(Cfg(), inputs)
```
